# Optimizing a Trainium2 kernel written in Bass

```python
import jax, jax.numpy as jnp
from jax import lax
import numpy as np

D_MODEL = 1024
BATCH = 8
SEQ = 8192
DEPTH = 1
DEC_BATCH = 2
DEC_SEQ = 8192
PAST_LEN = 128

D_MIX = D_MODEL
D_POOL = D_MIX // 2
D_CONV = D_MIX - D_POOL
POOL_WINDOWS = (2, 4, 8, 16)
N_POOL_GROUPS = len(POOL_WINDOWS)
POOL_GROUP = D_POOL // N_POOL_GROUPS
CONV_HEADS = 8
CONV_WIDTH = 3
D_PLE = 256
D_IN = 2 * D_POOL + 4 * D_CONV
EPS = 1e-6

kernel_name = "hybrid_pool_shortconv_encoder"


def _rmsnorm(x, g):
    xf = x.astype(jnp.float32)
    y = xf * lax.rsqrt(jnp.mean(xf * xf, axis=-1, keepdims=True) + EPS) * g.astype(jnp.float32)
    return y.astype(x.dtype)


def _pool_mixer(v, w_pool, scale):
    B, L, C = v.shape
    vf = v.astype(jnp.float32)
    cs = jnp.concatenate([jnp.zeros((B, 1, C), jnp.float32), jnp.cumsum(vf, axis=1)], axis=1)
    t = jnp.arange(L)
    outs = []
    for g, w in enumerate(POOL_WINDOWS):
        left, right = (w - 1) // 2, w // 2
        lo = jnp.clip(t - left, 0, L)
        hi = jnp.clip(t + right + 1, 0, L)
        c0 = g * POOL_GROUP
        csg = cs[:, :, c0:c0 + POOL_GROUP]
        mean = (csg[:, hi] - csg[:, lo]) / (hi - lo).astype(jnp.float32)[None, :, None]
        outs.append(mean - vf[:, :, c0:c0 + POOL_GROUP])
    d = jnp.stack(outs, axis=2).astype(v.dtype)
    y = jnp.einsum('blgc,gcd->blgd', d, w_pool).reshape(B, L, C)
    return y * scale


def _short_conv(u, gb, gc, conv_w):
    v = gc * u
    vp = jnp.pad(v, ((0, 0), (1, 1), (0, 0)))
    y = vp[:, :-2] * conv_w[0] + vp[:, 1:-1] * conv_w[1] + vp[:, 2:] * conv_w[2]
    return gb * y


def _layer(x, p, g_pre, w_in, w_pool, pool_scale, conv_w, w_out, g_post, w_ple, w_ple_gate, g_ple):
    h = _rmsnorm(x, g_pre)
    z = jnp.einsum('bsd,de->bse', h, w_in)
    a_val, a_gate, b_u, b_B, b_C, b_gate = jnp.split(z, 6, axis=-1)
    ya = _pool_mixer(a_val, w_pool, pool_scale) * jax.nn.silu(a_gate)
    yb = _short_conv(b_u, b_B, b_C, conv_w) * jax.nn.silu(b_gate)
    o = jnp.einsum('bse,ed->bsd', jnp.concatenate([ya, yb], axis=-1), w_out)
    x = x + _rmsnorm(o, g_post)
    gate = jax.nn.sigmoid(jnp.einsum('bsd,de->bse', x, w_ple_gate))
    e = jnp.einsum('bsk,kd->bsd', p, w_ple) * gate
    return x + _rmsnorm(e, g_ple)


def _trunk(x, p, g_pre, w_in, w_pool, pool_scale, conv_w, w_out, g_post, w_ple, w_ple_gate, g_ple):
    for i in range(DEPTH):
        x = _layer(x, p[i], g_pre[i], w_in[i], w_pool[i], pool_scale[i], conv_w[i], w_out[i],
                   g_post[i], w_ple[i], w_ple_gate[i], g_ple[i])
    return x


def setup_inputs(seed: int = 0) -> dict:
    key = jax.random.key(seed)
    ks = jax.random.split(key, 16)
    f32 = jnp.float32
    nrm = lambda k, s, sc: jax.random.normal(k, s, f32) * sc
    return {
        "x_prompt": nrm(ks[0], (BATCH, SEQ, D_MODEL), 1.0),
        "x_sample": nrm(ks[1], (DEC_BATCH, DEC_SEQ, D_MODEL), 1.0),
        "p_prompt": nrm(ks[2], (DEPTH, BATCH, SEQ, D_PLE), 1.0),
        "p_sample": nrm(ks[3], (DEPTH, DEC_BATCH, DEC_SEQ, D_PLE), 1.0),
        "g_pre": 1.0 + nrm(ks[4], (DEPTH, D_MODEL), 0.02),
        "w_in": nrm(ks[5], (DEPTH, D_MODEL, D_IN), D_MODEL ** -0.5),
        "w_pool": nrm(ks[6], (DEPTH, N_POOL_GROUPS, POOL_GROUP, POOL_GROUP), POOL_GROUP ** -0.5),
        "pool_scale": 1.0 + nrm(ks[7], (DEPTH, D_POOL), 0.1),
        "conv_w": nrm(ks[8], (DEPTH, CONV_WIDTH, D_CONV), CONV_WIDTH ** -0.5),
        "w_out": nrm(ks[9], (DEPTH, D_MIX, D_MODEL), D_MIX ** -0.5),
        "g_post": 1.0 + nrm(ks[10], (DEPTH, D_MODEL), 0.02),
        "w_ple": nrm(ks[11], (DEPTH, D_PLE, D_MODEL), D_PLE ** -0.5),
        "w_ple_gate": nrm(ks[12], (DEPTH, D_MODEL, D_MODEL), D_MODEL ** -0.5),
        "g_ple": 1.0 + nrm(ks[13], (DEPTH, D_MODEL), 0.02),
    }


def reference(x_prompt, x_sample, p_prompt, p_sample, g_pre, w_in, w_pool, pool_scale, conv_w,
              w_out, g_post, w_ple, w_ple_gate, g_ple):
    y_prompt = _trunk(x_prompt, p_prompt, g_pre, w_in, w_pool, pool_scale, conv_w, w_out,
                      g_post, w_ple, w_ple_gate, g_ple)
    y_sample = _trunk(x_sample, p_sample, g_pre, w_in, w_pool, pool_scale, conv_w, w_out,
                      g_post, w_ple, w_ple_gate, g_ple)
    return (y_prompt, y_sample)
```

```python
import contextlib
import numpy as np
import concourse.bass as bass
import concourse.mybir as mybir
from concourse.bass_utils import run_bass_kernel_spmd

F32 = mybir.dt.float32
BF16 = mybir.dt.bfloat16
ALU = mybir.AluOpType
AF = mybir.ActivationFunctionType

N_CORES = 8
D = 1024
KD = 8
DIN = 3072
DPLE = 256
SEQ = 8192
CHUNK = SEQ // N_CORES
TB = 512
HAL = 8
CB = TB + 2 * HAL
EPS = 1e-6
WINDOWS = (2, 4, 8, 16)


class Sched:
    COMPUTE = ("pe", "act", "dve", "pool")

    def __init__(self):
        self.streams = {e: [] for e in self.COMPUTE + ("sp",)}
        self.cnt = {e: 0 for e in self.COMPUTE}
        self.waited = {e: {} for e in self.COMPUTE + ("sp",)}
        self.lastw = {}
        self.readers = {}
        self.dmacnt = {}
        self.const_keys = set()

    def _need(self, eng, tok, raw):
        sem, val, teng = tok
        if self.waited[eng].get(sem, 0) < val:
            self.streams[eng].append(("wait", sem, val))
            self.waited[eng][sem] = val

    @staticmethod
    def _is_psum(k):
        return k in ("trp", "misc") or (isinstance(k, tuple) and k[0] in ("zb", "cps"))

    def op(self, eng, fn, reads=(), writes=(), dma=None, dma_val=None):
        writes = list(writes) + [k for k in reads if self._is_psum(k) and k not in writes]
        reads = [k for k in reads if not self._is_psum(k)]
        for k in reads:
            if k in self.lastw:
                self._need(eng, self.lastw[k], True)
        for k in writes:
            if k in self.lastw:
                self._need(eng, self.lastw[k], False)
            for sem, (val, teng) in self.readers.get(k, {}).items():
                self._need(eng, (sem, val, teng), False)
        if dma is None:
            self.cnt[eng] += 1
            tok = ("E_" + eng, self.cnt[eng], eng)
            inc = 1
        else:
            self.dmacnt[dma] = self.dmacnt.get(dma, 0) + 16
            tok = (dma, dma_val if dma_val is not None else self.dmacnt[dma], "dma")
            inc = 16
        self.streams[eng].append(("op", fn, tok[0], inc))
        for k in reads:
            if k in self.const_keys:
                continue
            r = self.readers.setdefault(k, {})
            if r.get(tok[0], (0, None))[0] < tok[1]:
                r[tok[0]] = (tok[1], tok[2])
        for k in writes:
            self.lastw[k] = tok
            self.readers[k] = {}
        return tok

    def final_wait(self, eng, sem, val):
        if self.waited[eng].get(sem, 0) < val:
            self.streams[eng].append(("wait", sem, val))
            self.waited[eng][sem] = val

    def sem_names(self):
        names = set()
        for st in self.streams.values():
            for it in st:
                if it[0] == "wait":
                    names.add(it[1])
                else:
                    names.add(it[2])
        return sorted(names)


def build_program(NS):
    NB = 2 * NS
    nc = bass.Bass("TRN2", target_bir_lowering=False)
    dram = {}

    def din(name, shape):
        dram[name] = nc.dram_tensor(name, list(shape), F32, kind="ExternalInput").ap()
        return dram[name]

    xm = din("xm", [NB * TB, D])
    xh = din("xh", [NB * 16, D])
    pm = din("pm", [NB * TB, DPLE])
    w_in = din("w_in", [D, DIN])
    w_pool = din("w_pool", [4 * 128, 128])
    w_out = din("w_out", [D, D])
    w_gate = din("w_gate", [D, D])
    w_ple = din("w_ple", [DPLE, D])
    gpre_d = din("gpre", [128, KD])
    pscale_d = din("pscale", [128, 4])
    cw_d = din("cw", [128, 12])
    gpost_d = din("gpost", [128, D])
    gple_d = din("gple", [128, D])
    valid_d = din("valid", [128, 96])
    ident_d = din("ident", [128, 128])
    y = nc.dram_tensor("y", [NB * TB, D], F32, kind="ExternalOutput").ap()

    S = Sched()
    es = contextlib.ExitStack()

    def sb(name, shape, dt):
        return es.enter_context(nc.sbuf_tensor("sb_" + name, list(shape), dt))

    def ps(name, shape, dt):
        return es.enter_context(nc.psum_tensor("ps_" + name, list(shape), dt))

    Win = sb("Win", [128, KD, DIN], BF16)
    Wout = sb("Wout", [128, KD, D], BF16)
    Wg = sb("Wg", [128, KD, D], BF16)
    Wple = sb("Wple", [128, 2, D], BF16)
    Wpool = sb("Wpool", [128, 4, 128], BF16)
    gpost = sb("gpost_bc", [128, D], F32)
    gple = sb("gple_bc", [128, D], F32)
    ident_f = sb("ident_f", [128, 128], F32)
    ident = sb("ident", [128, 128], BF16)
    gpre = sb("gpre_t", [128, KD], F32)
    pscale = sb("pscale_t", [128, 4], F32)
    cw = sb("cw_t", [128, 12], F32)
    valid = sb("valid_t", [128, 4, 24], F32)
    vs0 = sb("vs0", [128, 4, 24], F32)
    vs1 = sb("vs1", [128, 4, 24], F32)
    invc = sb("invc", [128, 4, 4, 8], F32)
    neghalf = sb("neghalf", [128, 1], F32)
    stat = sb("stat", [128, 64], F32)
    junk = sb("junk", [128, D], BF16)

    NXA = 3
    NXC = 3
    xa = [sb(f"xa{i}", [128, D], F32) for i in range(NXA)]
    xc = [sb(f"xc{i}", [128, D], F32) for i in range(NXC)]
    hb = [sb(f"hb{i}", [128, D], BF16) for i in range(2)]
    hT = [sb(f"hT{i}", [128, KD, CB], BF16) for i in range(2)]
    abuf = [sb(f"a{i}", [128, CB], F32) for i in range(2)]
    sbuf_ = [sb(f"s{i}", [128, CB], F32) for i in range(2)]
    ccbuf = [sb(f"cc{i}", [128, CB], F32) for i in range(1)]
    vbuf = [sb(f"v{i}", [128, CB], F32) for i in range(2)]
    tbuf = [sb(f"t{i}", [128, TB], F32) for i in range(2)]
    etmp = sb("etmp", [128, 2, 8], F32)
    zhsb = [sb(f"zhsb{i}", [128, 12, 16], F32) for i in range(2)]
    dT = [sb(f"dT{i}", [128, TB], BF16) for i in range(2)]
    NSG = 3
    sg = [sb(f"sg{i}", [128, TB], F32) for i in range(NSG)]
    yT = [sb(f"yT{i}", [128, KD, TB], BF16) for i in range(2)]
    tmpc = sb("tmpc", [128, D], F32)
    x1b = [sb(f"x1b{i}", [128, D], BF16) for i in range(2)]
    x1T = [sb(f"x1T{i}", [128, KD, 128], BF16) for i in range(2)]
    th = tmpc
    NE = 1
    ebuf = [sb(f"e{i}", [128, D], F32) for i in range(NE)]
    pt = [sb(f"pt{i}", [128, DPLE], F32) for i in range(2)]
    pb = [sb(f"pb{i}", [128, DPLE], BF16) for i in range(2)]
    pT = [sb(f"pT{i}", [128, 2, 128], BF16) for i in range(2)]

    trp = ps("trp", [128, D], BF16)
    misc = ps("misc", [128, 512], F32)
    NZ = 4
    zb = [ps(f"zb{i}", [128, TB], F32) for i in range(NZ)]
    cps = ps("cps", [128, D], F32)
    zh = misc[:, 0:192].rearrange("p (i c) -> p i c", c=16)
    pTps = misc[:, 256:384].bitcast(BF16)

    state = {"z": 0, "stat": 0, "stg": 0}

    def next_z():
        i = state["z"] % NZ
        state["z"] += 1
        return i

    def stat_col():
        i = state["stat"] % 64
        state["stat"] += 1
        return i

    N_CONST = 7
    cst_total = 16 * N_CONST

    def cdma(dst, src, key):
        S.op("sp", lambda e, dst=dst, src=src: e.dma_start(out=dst, in_=src),
             writes=[key], dma="D_cst", dma_val=cst_total)

    cdma(gpre[:], gpre_d[:, :], "gpre")
    cdma(ident_f[:], ident_d[:, :], "ident_f")
    cdma(valid[:].rearrange("p a b -> p (a b)"), valid_d[:, :], "valid")
    cdma(pscale[:], pscale_d[:, :], "pscale")
    cdma(cw[:], cw_d[:, :], "cw")
    cdma(gpost[:], gpost_d[:, :], "gpost")
    cdma(gple[:], gple_d[:, :], "gple")
    S.op("pool", lambda e: e.memset(neghalf[:], -0.5), writes=["neghalf"])
    S.op("dve", lambda e: e.tensor_copy(out=ident[:], in_=ident_f[:]), reads=["ident_f"], writes=["ident"])

    def doubling(eng_name, src3, bufs3, n, g, key_src, key_bufs):
        cur, cur_key, length = src3, key_src, n
        for lvl in range(g + 1):
            sh = 1 << lvl
            dst = bufs3[lvl % 2]
            dk = key_bufs[lvl % 2]
            nl = length - sh
            S.op(eng_name,
                 lambda e, dst=dst, cur=cur, nl=nl, sh=sh: e.tensor_tensor(
                     out=dst(0, nl), in0=cur(0, nl), in1=cur(sh, sh + nl), op=ALU.add),
                 reads=[cur_key], writes=[dk])
            cur, cur_key, length = dst, dk, nl
        return cur, cur_key

    for g, w in enumerate(WINDOWS):
        left = (w - 1) // 2
        src = lambda a, b: valid[:, :, a:b]
        b0 = lambda a, b: vs0[:, :, a:b]
        b1 = lambda a, b: vs1[:, :, a:b]
        cur, ck = doubling("dve", src, [b0, b1], 24, g, "valid", ["vs0", "vs1"])
        S.op("dve", lambda e, cur=cur, g=g, left=left: e.reciprocal(
            out=invc[:, g, :, :], in_=cur(8 - left, 16 - left)), reads=[ck], writes=["invc"])

    cast_rr = {"i": 0}

    def WK(name):
        return [(name, en) for en in ("dve", "pool", "act")]

    def stage_weight(wname, src_ap, dst_ap, rows, cols, scale_ap=None, wide=False):
        pool_ = [("xc", i) for i in range(NXC)] + ([("xa", i) for i in range(NXA)] if wide else [])
        kind, slot = pool_[state["stg"] % len(pool_)]
        state["stg"] += 1
        stg = (xc if kind == "xc" else xa)[slot]
        S.op("sp", lambda e, stg=stg, src_ap=src_ap, rows=rows, cols=cols: e.dma_start(
            out=stg[0:rows, 0:cols], in_=src_ap), writes=[(kind, slot)], dma=f"D_{kind}{slot}")
        eng = ("dve", "pool", "act")[cast_rr["i"] % 3]
        cast_rr["i"] += 1
        rd = [(kind, slot)] + (["gpre"] if scale_ap is not None else [])
        if scale_ap is not None:
            if eng == "act":
                fn = lambda e, stg=stg, dst_ap=dst_ap, rows=rows, cols=cols, scale_ap=scale_ap: e.activation(
                    out=dst_ap, in_=stg[0:rows, 0:cols], func=AF.Copy, scale=scale_ap)
            else:
                fn = lambda e, stg=stg, dst_ap=dst_ap, rows=rows, cols=cols, scale_ap=scale_ap: e.tensor_scalar(
                    out=dst_ap, in0=stg[0:rows, 0:cols], scalar1=scale_ap, scalar2=1.0, op0=ALU.mult, op1=ALU.mult)
        else:
            if eng == "act":
                fn = lambda e, stg=stg, dst_ap=dst_ap, rows=rows, cols=cols: e.activation(
                    out=dst_ap, in_=stg[0:rows, 0:cols], func=AF.Copy)
            else:
                fn = lambda e, stg=stg, dst_ap=dst_ap, rows=rows, cols=cols: e.tensor_copy(
                    out=dst_ap, in_=stg[0:rows, 0:cols])
        S.op(eng, fn, reads=rd, writes=[(wname, eng)])

    def stage_early_weights():
        for n in (2, 1, 0):
            for k in range(KD):
                stage_weight(f"Win{n}", w_in[k * 128:(k + 1) * 128, n * 1024:(n + 1) * 1024],
                             Win[:, k, n * 1024:(n + 1) * 1024], 128, 1024, gpre[:, k:k + 1], wide=True)
        for g in range(4):
            stage_weight("Wpool", w_pool[g * 128:(g + 1) * 128, :], Wpool[:, g, :], 128, 128, wide=True)
        state["stg"] = 0

    late_weights = []
    for k in range(KD):
        late_weights.append(("Wout", w_out[k * 128:(k + 1) * 128, :], Wout[:, k, :], 128, 1024))
    for k in range(KD):
        late_weights.append(("Wg", w_gate[k * 128:(k + 1) * 128, :], Wg[:, k, :], 128, 1024))
    for k in range(2):
        late_weights.append(("Wple", w_ple[k * 128:(k + 1) * 128, :], Wple[:, k, :], 128, 1024))
    for _wn in ("Win0", "Win1", "Win2", "Wpool", "Wout", "Wg", "Wple"):
        S.const_keys.update(WK(_wn))
    S.const_keys.update(["gpre", "pscale", "cw", "gpost", "gple", "ident", "invc", "neghalf"])

    cnt = {"xa": 0, "hb": 0, "xc": 0, "pt": 0, "pb": 0, "x1b": 0, "e": 0, "ab": 0, "vb": 0,
           "dT": 0, "sg": 0}

    def rot(name, n):
        i = cnt[name] % n
        cnt[name] += 1
        return i

    def rstd_from(ss_ap, ss_key, eps_val, extra=None):
        c1 = stat_col()
        c2 = stat_col()
        m_ap = stat[:, c1:c1 + 1]
        r_ap = stat[:, c2:c2 + 1]
        mk, rk = ("stat", c1), ("stat", c2)
        if extra is None:
            S.op("pool", lambda e: e.tensor_scalar(out=m_ap, in0=ss_ap, scalar1=float(eps_val), scalar2=None,
                                                   op0=ALU.add), reads=[ss_key], writes=[mk])
        else:
            ex_ap, ex_key = extra
            S.op("pool", lambda e: e.tensor_scalar(out=m_ap, in0=ss_ap, scalar1=ex_ap, scalar2=float(eps_val),
                                                   op0=ALU.add, op1=ALU.add), reads=[ss_key, ex_key], writes=[mk])
        S.op("pool", lambda e: e.tensor_tensor(out=r_ap, in0=m_ap, in1=neghalf[:], op=ALU.pow),
             reads=[mk, "neghalf"], writes=[rk])
        return r_ap, rk

    actx = {}

    aload = {}
    cload = {}

    def A_load(b, j):
        rows = 128 if j < 4 else 16
        xs = rot("xa", NXA)
        xt = xa[xs]
        src = xm[b * TB + j * 128: b * TB + (j + 1) * 128, :] if j < 4 else xh[b * 16:(b + 1) * 16, :]
        S.op("sp", lambda e: e.dma_start(out=xt[0:rows, :], in_=src), writes=[("xa", xs)], dma=f"D_xa{xs}")
        aload[(b, j)] = xs

    def C_load(b, j):
        r0 = b * TB + j * 128
        xs = rot("xc", NXC)
        pi = rot("pt", 2)
        S.op("sp", lambda e: e.dma_start(out=xc[xs][:, :], in_=xm[r0:r0 + 128, :]), writes=[("xc", xs)],
             dma=f"D_xc{xs}")
        S.op("sp", lambda e: e.dma_start(out=pt[pi][:, :], in_=pm[r0:r0 + 128, :]), writes=[("pt", pi)],
             dma=f"D_pt{pi}")
        cload[(b, j)] = (xs, pi)

    def stage_A1(b, j):
        rows = 128 if j < 4 else 16
        xs = aload.pop((b, j))
        xt = xa[xs]
        c0 = stat_col()
        ss_ap, ssk = stat[0:rows, c0:c0 + 1], ("stat", c0)
        S.op("act", lambda e: e.activation(out=junk[0:rows, :], in_=xt[0:rows, :], func=AF.Square,
                                           scale=1.0 / 32.0, accum_out=ss_ap),
             reads=[("xa", xs)], writes=[ssk, ("junk", 0), ("junk", 1)])
        c1, c2 = stat_col(), stat_col()
        m_ap, r_ap = stat[0:rows, c1:c1 + 1], stat[0:rows, c2:c2 + 1]
        mk, rk = ("stat", c1), ("stat", c2)
        S.op("pool", lambda e: e.tensor_scalar(out=m_ap, in0=ss_ap, scalar1=float(EPS), scalar2=None,
                                               op0=ALU.add), reads=[ssk], writes=[mk])
        S.op("pool", lambda e: e.tensor_tensor(out=r_ap, in0=m_ap, in1=neghalf[0:rows, :], op=ALU.pow),
             reads=[mk, "neghalf"], writes=[rk])
        actx[(b, j)] = dict(xs=xs, r_ap=r_ap, rk=rk)

    a2ctx = {}

    def stage_A2a(b, j):
        hs = b % 2
        rows = 128 if j < 4 else 16
        c = actx.pop((b, j))
        xs, r_ap, rk = c["xs"], c["r_ap"], c["rk"]
        xt = xa[xs]
        hi = rot("hb", 2)
        hbt = hb[hi]
        S.op("act", lambda e: e.activation(out=hbt[0:rows, :], in_=xt[0:rows, :], func=AF.Copy, scale=r_ap),
             reads=[("xa", xs), rk], writes=[("hb", hi)])
        a2ctx[(b, j)] = hi

    def stage_A2b(b, j):
        hs = b % 2
        rows = 128 if j < 4 else 16
        hi = a2ctx.pop((b, j))
        hbt = hb[hi]

        def tr(e):
            ins = None
            for k in range(KD):
                ins = e.transpose(out=trp[:, k * 128:k * 128 + rows],
                                  in_=hbt[0:rows, k * 128:(k + 1) * 128], identity=ident[0:rows, 0:rows])
            return ins
        S.op("pe", tr, reads=[("hb", hi), "ident"], writes=["trp"])
        src_v = trp[:, :].rearrange("p (k t) -> p k t", t=128)[:, :, 0:rows]
        if j < 4:
            dst_v = hT[hs][:, :, j * 128:(j + 1) * 128]
        else:
            dst_v = hT[hs][:, :, TB:TB + 16]
        S.op("dve", lambda e: e.tensor_copy(out=dst_v, in_=src_v), reads=["trp"], writes=[("hT", hs, j)])

    HT_KEYS = lambda hs: [("hT", hs, j) for j in range(5)]

    def zgroup(b, ec):
        hs = b % 2
        zi = next_z()

        def mm(e):
            ins = None
            for k in range(KD):
                ins = e.matmul(zb[zi][:, :], lhsT=Win[:, k, ec * 128:(ec + 1) * 128], rhs=hT[hs][:, k, 0:TB],
                               start=(k == 0), stop=(k == KD - 1))
            return ins
        S.op("pe", mm, reads=WK(f"Win{ec // 8}") + HT_KEYS(hs)[:4], writes=[("zb", zi)])
        return zi

    HALO_EC = [0, 1, 2, 3, 8, 9, 10, 11, 16, 17, 18, 19]

    def stage_B_halo(b):
        hs = b % 2

        def mmh(e):
            ins = None
            for idx, ec in enumerate(HALO_EC):
                for k in range(KD):
                    ins = e.matmul(zh[:, idx, :], lhsT=Win[:, k, ec * 128:(ec + 1) * 128],
                                   rhs=hT[hs][:, k, TB:TB + 16], start=(k == 0), stop=(k == KD - 1))
            return ins
        S.op("pe", mmh, reads=WK("Win0") + WK("Win1") + WK("Win2") + [("hT", hs, 4)], writes=["misc"])
        S.op("act", lambda e: e.activation(out=zhsb[hs][:, :, :], in_=zh[:, :, :], func=AF.Copy),
             reads=["misc"], writes=[("zhsb", hs)])

    def halo_view(hs, idx):
        return zhsb[hs][:, idx, :].rearrange("p (a c) -> p a c", c=8)

    def edges_nat(buf):
        return buf[:, :].rearrange("p (a c) -> p a c", c=8)[:, 0:66:65, :]

    def edges_main(ap512):
        return ap512.rearrange("p (a c) -> p a c", c=8)[:, 0:64:63, :]

    pctx = {}

    def stage_B_pool1a(b, g):
        hs = b % 2
        h = b % 2
        w = WINDOWS[g]
        left = (w - 1) // 2
        za = zgroup(b, g)
        ai = rot("ab", 2)
        a = abuf[ai]
        ak = ("a", ai)
        S.op("act", lambda e: e.activation(out=a[:, HAL:HAL + TB], in_=zb[za][:, :], func=AF.Copy),
             reads=[("zb", za)], writes=[ak])
        S.op("pool", lambda e: e.tensor_copy(out=edges_nat(a), in_=halo_view(hs, g)),
             reads=[("zhsb", hs)], writes=[(ak, "e")])
        src = lambda lo, hi: a[:, lo:hi]
        b0 = lambda lo, hi: sbuf_[0][:, lo:hi]
        b1 = lambda lo, hi: sbuf_[1][:, lo:hi]
        cur, cur_key, length = src, None, CB
        first = True
        for lvl in range(g + 1):
            sh = 1 << lvl
            dst = (b0, b1)[lvl % 2]
            dk = ("s", lvl % 2)
            nl = length - sh
            rd = [ak, (ak, "e")] if first else [cur_key]
            S.op("dve", lambda e, dst=dst, cur=cur, nl=nl, sh=sh: e.tensor_tensor(
                out=dst(0, nl), in0=cur(0, nl), in1=cur(sh, sh + nl), op=ALU.add), reads=rd, writes=[dk])
            cur, cur_key, length, first = dst, dk, nl, False
        o = HAL - left
        di = rot("dT", 2)
        S.op("dve", lambda e: e.scalar_tensor_tensor(out=dT[di][:, :], in0=cur(o, o + TB), scalar=1.0 / w,
                                                     in1=a[:, HAL:HAL + TB], op0=ALU.mult, op1=ALU.subtract),
             reads=[cur_key, ak], writes=[("dT", di)])
        S.op("pool", lambda e: e.tensor_tensor(out=etmp[:, :, :], in0=edges_main(cur(o, o + TB)),
                                               in1=invc[:, g, 2 * h:2 * h + 2, :], op=ALU.mult),
             reads=[cur_key, "invc"], writes=["etmp"])
        S.op("pool", lambda e: e.tensor_tensor(out=edges_main(dT[di][:, :]), in0=etmp[:, :, :],
                                               in1=edges_main(a[:, HAL:HAL + TB]), op=ALU.subtract),
             reads=["etmp", ak, ("dT", di)], writes=[("dT", di)])
        pctx[(b, g)] = di

    def stage_B_pool1b(b, g):
        di = pctx[(b, g)]
        zg = zgroup(b, 4 + g)
        si = rot("sg", NSG)
        S.op("act", lambda e: e.activation(out=sg[si][:, :], in_=zb[zg][:, :], func=AF.Silu),
             reads=[("zb", zg)], writes=[("sg", si)])
        pctx[(b, g)] = (di, si)

    def stage_B_pool2(b, g):
        hs = b % 2
        di, si = pctx.pop((b, g))
        zp = next_z()
        S.op("pe", lambda e: e.matmul(zb[zp][:, :], lhsT=Wpool[:, g, :], rhs=dT[di][:, :], start=True, stop=True),
             reads=WK("Wpool") + [("dT", di)], writes=[("zb", zp)])
        S.op("dve", lambda e: e.scalar_tensor_tensor(out=yT[hs][:, g, :], in0=zb[zp][:, :],
                                                     scalar=pscale[:, g:g + 1], in1=sg[si][:, :],
                                                     op0=ALU.mult, op1=ALU.mult),
             reads=[("zb", zp), ("sg", si), "pscale"], writes=[("yT", hs, g)])

    vctx = {}

    def stage_B_conv_a(b, g):
        hs = b % 2
        zc = zgroup(b, 16 + g)
        cc = ccbuf[0]
        S.op("act", lambda e: e.activation(out=cc[:, HAL:HAL + TB], in_=zb[zc][:, :], func=AF.Copy),
             reads=[("zb", zc)], writes=["cc"])
        zu = zgroup(b, 8 + g)
        vi = rot("vb", 2)
        v = vbuf[vi]
        vk = ("v", vi)
        S.op("pool", lambda e: e.tensor_tensor(out=edges_nat(v), in0=halo_view(hs, 4 + g), in1=halo_view(hs, 8 + g),
                                               op=ALU.mult), reads=[("zhsb", hs)], writes=[(vk, "e")])
        vctx[(b, g)] = dict(zu=zu, vi=vi)

    def stage_B_conv_b1(b, g):
        c = vctx[(b, g)]
        zu, vi = c["zu"], c["vi"]
        cc = ccbuf[0]
        v = vbuf[vi]
        vk = ("v", vi)
        S.op("dve", lambda e: e.tensor_tensor(out=v[:, HAL:HAL + TB], in0=zb[zu][:, :], in1=cc[:, HAL:HAL + TB],
                                              op=ALU.mult), reads=[("zb", zu), "cc"], writes=[vk])
        w0, w1, w2 = (cw[:, 3 * g + i:3 * g + i + 1] for i in range(3))
        S.op("dve", lambda e: e.tensor_scalar(out=tbuf[0][:, :], in0=v[:, HAL:HAL + TB], scalar1=w1, scalar2=None,
                                              op0=ALU.mult), reads=[vk, "cw"], writes=[("t", 0)])
        S.op("dve", lambda e: e.scalar_tensor_tensor(out=tbuf[1][:, :], in0=v[:, HAL - 1:HAL - 1 + TB], scalar=w0,
                                                     in1=tbuf[0][:, :], op0=ALU.mult, op1=ALU.add),
             reads=[vk, (vk, "e"), ("t", 0), "cw"], writes=[("t", 1)])
        S.op("dve", lambda e: e.scalar_tensor_tensor(out=tbuf[0][:, :], in0=v[:, HAL + 1:HAL + 1 + TB], scalar=w2,
                                                     in1=tbuf[1][:, :], op0=ALU.mult, op1=ALU.add),
             reads=[vk, (vk, "e"), ("t", 1), "cw"], writes=[("t", 0)])

    def stage_B_conv_b2(b, g):
        hs = b % 2
        vctx.pop((b, g))
        zg = zgroup(b, 20 + g)
        si = rot("sg", NSG)
        S.op("act", lambda e: e.activation(out=sg[si][:, :], in_=zb[zg][:, :], func=AF.Silu),
             reads=[("zb", zg)], writes=[("sg", si)])
        zB = zgroup(b, 12 + g)
        S.op("dve", lambda e: e.tensor_tensor(out=tbuf[1][:, :], in0=zb[zB][:, :], in1=tbuf[0][:, :], op=ALU.mult),
             reads=[("zb", zB), ("t", 0)], writes=[("t", 1)])
        S.op("pool", lambda e: e.tensor_tensor(out=yT[hs][:, 4 + g, :], in0=tbuf[1][:, :], in1=sg[si][:, :],
                                               op=ALU.mult),
             reads=[("t", 1), ("sg", si)], writes=[("yT", hs, 4 + g)])

    YT_KEYS = lambda hs: [("yT", hs, g) for g in range(8)]
    cctx = {}

    def stage_C1a(b, j):
        hs = b % 2
        r0 = b * TB + j * 128
        xs, pi = cload.pop((b, j))
        xt = xc[xs]
        sscols = []
        for hf in range(2):
            def mm(e, hf=hf):
                ins = None
                for k in range(KD):
                    ins = e.matmul(cps[:, hf * 512:(hf + 1) * 512], lhsT=yT[hs][:, k, j * 128:(j + 1) * 128],
                                   rhs=Wout[:, k, hf * 512:(hf + 1) * 512], start=(k == 0), stop=(k == KD - 1))
                return ins
            S.op("pe", mm, reads=WK("Wout") + YT_KEYS(hs), writes=[("cps", hf)])
        for hf in range(2):
            c0 = stat_col()
            ss_ap, ssk = stat[:, c0:c0 + 1], ("stat", c0)
            S.op("act", lambda e, hf=hf, ss_ap=ss_ap: e.activation(
                out=junk[:, hf * 512:(hf + 1) * 512], in_=cps[:, hf * 512:(hf + 1) * 512], func=AF.Square,
                scale=1.0 / 32.0, accum_out=ss_ap), reads=[("cps", hf)], writes=[ssk, ("junk", hf)])
            sscols.append((ss_ap, ssk))
            sl = slice(hf * 512, (hf + 1) * 512)
            S.op("dve", lambda e, sl=sl: e.tensor_tensor(out=tmpc[:, sl], in0=cps[:, sl], in1=gpost[:, sl],
                                                        op=ALU.mult),
                 reads=[("cps", hf), "gpost"], writes=[("tmpc", hf)])
        r_ap, rk = rstd_from(sscols[0][0], sscols[0][1], EPS, extra=sscols[1])
        cctx[(b, j)] = dict(xs=xs, pi=pi, r0=r0, r_ap=r_ap, rk=rk)

    def stage_C1b(b, j):
        c = cctx[(b, j)]
        xs, r_ap, rk = c["xs"], c["r_ap"], c["rk"]
        xt = xc[xs]
        S.op("dve", lambda e: e.scalar_tensor_tensor(out=xt[:, :], in0=tmpc[:, :], scalar=r_ap, in1=xt[:, :],
                                                     op0=ALU.mult, op1=ALU.add),
             reads=[("tmpc", 0), ("tmpc", 1), rk, ("xc", xs)], writes=[("xc", xs)])
        xi = rot("x1b", 2)
        S.op("act", lambda e: e.activation(out=x1b[xi][:, :], in_=xt[:, :], func=AF.Copy), reads=[("xc", xs)],
             writes=[("x1b", xi)])
        c["xi"] = xi
        pi = c["pi"]
        bi = rot("pb", 2)
        S.op("act", lambda e: e.activation(out=pb[bi][:, :], in_=pt[pi][:, :], func=AF.Copy), reads=[("pt", pi)],
             writes=[("pb", bi)])
        c["bi"] = bi

    def stage_C2(b, j):
        c = cctx[(b, j)]
        xi, bi = c["xi"], c["bi"]

        def tr(e):
            ins = None
            for k in range(KD):
                ins = e.transpose(out=trp[:, k * 128:(k + 1) * 128], in_=x1b[xi][:, k * 128:(k + 1) * 128],
                                  identity=ident[:, :])
            return ins
        S.op("pe", tr, reads=[("x1b", xi), "ident"], writes=["trp"])
        S.op("act", lambda e: e.activation(out=x1T[xi][:, :, :].rearrange("p k t -> p (k t)"), in_=trp[:, :],
                                           func=AF.Copy), reads=["trp"], writes=[("x1T", xi)])

        def trp2(e):
            ins = None
            for k in range(2):
                ins = e.transpose(out=pTps[:, k * 128:(k + 1) * 128], in_=pb[bi][:, k * 128:(k + 1) * 128],
                                  identity=ident[:, :])
            return ins
        S.op("pe", trp2, reads=[("pb", bi), "ident"], writes=["misc"])
        S.op("dve", lambda e: e.tensor_copy(out=pT[bi][:, :, :].rearrange("p k t -> p (k t)"), in_=pTps[:, :]),
             reads=["misc"], writes=[("pT", bi)])

    def stage_C3a(b, j):
        c = cctx.pop((b, j))
        xs, xi, bi, r0 = c["xs"], c["xi"], c["bi"], c["r0"]
        xt = xc[xs]
        for hf in range(2):
            def mmg(e, hf=hf):
                ins = None
                for k in range(KD):
                    ins = e.matmul(cps[:, hf * 512:(hf + 1) * 512], lhsT=x1T[xi][:, k, :],
                                   rhs=Wg[:, k, hf * 512:(hf + 1) * 512], start=(k == 0), stop=(k == KD - 1))
                return ins
            S.op("pe", mmg, reads=WK("Wg") + [("x1T", xi)], writes=[("cps", hf)])
        for hf in range(2):
            sl = slice(hf * 512, (hf + 1) * 512)
            S.op("act", lambda e, sl=sl: e.activation(out=th[:, sl], in_=cps[:, sl], func=AF.Tanh, scale=0.5),
                 reads=[("cps", hf)], writes=[("tmpc", hf)])
        cctx[(b, j)] = c

    def stage_C3b(b, j):
        c = cctx.pop((b, j))
        xs, xi, bi, r0 = c["xs"], c["xi"], c["bi"], c["r0"]
        xt = xc[xs]
        for hf in range(2):
            def mmp(e, hf=hf):
                ins = None
                for k in range(2):
                    ins = e.matmul(cps[:, hf * 512:(hf + 1) * 512], lhsT=pT[bi][:, k, :],
                                   rhs=Wple[:, k, hf * 512:(hf + 1) * 512], start=(k == 0), stop=(k == 1))
                return ins
            S.op("pe", mmp, reads=WK("Wple") + [("pT", bi)], writes=[("cps", hf)])
        ei = rot("e", NE)
        eb = ebuf[ei]
        sscols = []
        for hf in range(2):
            sl = slice(hf * 512, (hf + 1) * 512)
            S.op("dve", lambda e, sl=sl: e.scalar_tensor_tensor(out=eb[:, sl], in0=th[:, sl], scalar=1.0,
                                                               in1=cps[:, sl], op0=ALU.add, op1=ALU.mult),
                 reads=[("tmpc", hf), ("cps", hf)], writes=[("e", ei, hf)])
            c0 = stat_col()
            ss_ap, ssk = stat[:, c0:c0 + 1], ("stat", c0)
            S.op("act", lambda e, sl=sl, ss_ap=ss_ap, hf=hf: e.activation(
                out=junk[:, sl], in_=eb[:, sl], func=AF.Square, scale=1.0 / 32.0, accum_out=ss_ap),
                reads=[("e", ei, hf)], writes=[ssk, ("junk", hf)])
            sscols.append((ss_ap, ssk))
        r_ap, rk = rstd_from(sscols[0][0], sscols[0][1], 4.0 * EPS, extra=sscols[1])
        for hf in range(2):
            sl = slice(hf * 512, (hf + 1) * 512)
            S.op("dve", lambda e, sl=sl: e.tensor_tensor(out=eb[:, sl], in0=eb[:, sl], in1=gple[:, sl], op=ALU.mult),
                 reads=[("e", ei, hf), "gple"], writes=[("e", ei, hf)])
        S.op("dve", lambda e: e.scalar_tensor_tensor(out=xt[:, :], in0=eb[:, :], scalar=r_ap, in1=xt[:, :],
                                                     op0=ALU.mult, op1=ALU.add),
             reads=[("e", ei, 0), ("e", ei, 1), rk, ("xc", xs)], writes=[("xc", xs)])
        S.op("sp", lambda e: e.dma_start(out=y[r0:r0 + 128, :], in_=xt[:, :]), reads=[("xc", xs)],
             dma=f"D_st{xs}", writes=[("ystore", xs)])

    def a_tiles(b, j):
        return [(b, 4), (b, 0)] if j == 0 else [(b, j)]

    for t in [(0, 4), (0, 0), (0, 1), (0, 2), (0, 3)]:
        A_load(*t)
        stage_A1(*t)
        stage_A2a(*t)
        stage_A2b(*t)
    stage_early_weights()
    if NB > 1:
        for t in a_tiles(1, 0):
            A_load(*t)
    for s in range(NB + 1):
        hasA, hasB, hasC = (s + 1 < NB), (s < NB), (s >= 1)
        if s == NB:
            b = NB - 1
            stage_C1a(b, 0)
            stage_C1b(b, 0)
            for j in range(4):
                if j < 3:
                    C_load(b, j + 1)
                    stage_C1a(b, j + 1)
                    stage_C1b(b, j + 1)
                stage_C2(b, j)
                stage_C3a(b, j)
                stage_C3b(b, j)
            break
        for j in range(4):
            now = a_tiles(s + 1, j) if hasA else []
            if j < 3:
                nxt = a_tiles(s + 1, j + 1) if hasA else []
            else:
                nxt = a_tiles(s + 2, 0) if s + 2 < NB else []
            for t in nxt:
                A_load(*t)
            if j < 3:
                if hasC:
                    C_load(s - 1, j + 1)
            elif 0 < s < NB:
                C_load(s, 0)
            for t in now:
                stage_A1(*t)
            if hasB and j == 0:
                stage_B_halo(s)
            if hasB:
                stage_B_conv_a(s, j)
                stage_B_conv_b1(s, j)
            if hasC:
                stage_C1a(s - 1, j)
            for t in now:
                stage_A2a(*t)
            if hasC:
                stage_C1b(s - 1, j)
            if hasB:
                stage_B_conv_b2(s, j)
            if s == 0:
                for wargs in late_weights[j * 6:(j + 1) * 6]:
                    stage_weight(*wargs)
            if not hasC:
                for t in now:
                    stage_A2b(*t)
            if hasB:
                stage_B_pool1a(s, j)
            if hasC:
                stage_C2(s - 1, j)
            if hasB:
                stage_B_pool1b(s, j)
            if hasC:
                stage_C3a(s - 1, j)
                for t in now:
                    stage_A2b(*t)
                stage_C3b(s - 1, j)
            if hasB:
                stage_B_pool2(s, j)
            if s == 0 and j == 3 and NB > 0:
                C_load(0, 0)

    for name, val in sorted(S.dmacnt.items()):
        if name.startswith("D_st"):
            S.final_wait("sp", name, val)

    sems = {}
    for name in S.sem_names():
        sems[name] = es.enter_context(nc.semaphore(name))

    def replay(eng_name, eng):
        for it in S.streams[eng_name]:
            if it[0] == "wait":
                eng.wait_ge(sems[it[1]], it[2])
            else:
                ins = it[1](eng)
                ins.then_inc(sems[it[2]], it[3])

    with nc.Block() as block:
        @block.sync
        def _(e):
            replay("sp", e)

        @block.tensor
        def _(e):
            replay("pe", e)

        @block.scalar
        def _(e):
            replay("act", e)

        @block.vector
        def _(e):
            replay("dve", e)

        @block.gpsimd
        def _(e):
            replay("pool", e)

    es.close()
    return nc


_PROG_CACHE = {}


def _get_program(NS):
    if NS not in _PROG_CACHE:
        _PROG_CACHE[NS] = build_program(NS)
    return _PROG_CACHE[NS]


def _run(x_all, p_all, g_pre, w_in, w_pool, pool_scale, conv_w, w_out, g_post, w_ple, w_ple_gate, g_ple):
    NS = x_all.shape[0]
    NB = 2 * NS
    f32 = np.float32
    nc = _get_program(NS)
    shared = {
        "w_in": np.ascontiguousarray(w_in[0], dtype=f32),
        "w_pool": np.ascontiguousarray(w_pool[0].reshape(4 * 128, 128), dtype=f32),
        "w_out": np.ascontiguousarray(w_out[0], dtype=f32),
        "w_gate": np.ascontiguousarray(w_ple_gate[0], dtype=f32),
        "w_ple": np.ascontiguousarray(w_ple[0], dtype=f32),
        "gpre": np.ascontiguousarray(g_pre[0].reshape(KD, 128).T, dtype=f32),
        "pscale": np.ascontiguousarray(pool_scale[0].reshape(4, 128).T, dtype=f32),
        "cw": np.ascontiguousarray(conv_w[0].reshape(3, 4, 128).transpose(2, 1, 0).reshape(128, 12), dtype=f32),
        "gpost": np.ascontiguousarray(np.broadcast_to(g_post[0][None, :], (128, D)), dtype=f32),
        "gple": np.ascontiguousarray(np.broadcast_to(g_ple[0][None, :], (128, D)), dtype=f32),
        "ident": np.eye(128, dtype=f32),
    }
    in_maps = []
    for c in range(N_CORES):
        xm = np.ascontiguousarray(x_all[:, c * CHUNK:(c + 1) * CHUNK, :]).reshape(NS * CHUNK, D)
        pm = np.ascontiguousarray(p_all[:, c * CHUNK:(c + 1) * CHUNK, :]).reshape(NS * CHUNK, DPLE)
        xh = np.zeros((NS, 2, 16, D), dtype=f32)
        valid = np.zeros((4, 24), dtype=f32)
        for h in range(2):
            start = c * CHUNK + h * TB
            pos = np.concatenate([np.arange(start - HAL, start), np.arange(start + TB, start + TB + HAL)])
            ok = (pos >= 0) & (pos < SEQ)
            if ok.any():
                xh[:, h, ok, :] = x_all[:, pos[ok], :]
            for side in range(2):
                nat = np.arange(0, 24) if side == 0 else np.arange(CB - 24, CB)
                pp = start + nat - HAL
                valid[2 * h + side] = ((pp >= 0) & (pp < SEQ)).astype(f32)
        m = dict(shared)
        m["xm"] = xm
        m["pm"] = pm
        m["xh"] = xh.reshape(NB * 16, D)
        m["valid"] = np.ascontiguousarray(np.broadcast_to(valid.reshape(1, 96), (128, 96)), dtype=f32)
        in_maps.append(m)
    res = run_bass_kernel_spmd(nc, in_maps, core_ids=list(range(N_CORES)))
    y_all = np.empty((NS, SEQ, D), dtype=f32)
    for c in range(N_CORES):
        y_all[:, c * CHUNK:(c + 1) * CHUNK, :] = np.asarray(res.results[c]["y"]).reshape(NS, CHUNK, D)
    return y_all


def kernel(x_prompt, x_sample, p_prompt, p_sample, g_pre, w_in, w_pool, pool_scale, conv_w, w_out, g_post,
           w_ple, w_ple_gate, g_ple):
    x_prompt = np.asarray(x_prompt, dtype=np.float32)
    x_sample = np.asarray(x_sample, dtype=np.float32)
    nb = x_prompt.shape[0]
    x_all = np.concatenate([x_prompt, x_sample], axis=0)
    p_all = np.concatenate([np.asarray(p_prompt, dtype=np.float32)[0], np.asarray(p_sample, dtype=np.float32)[0]],
                           axis=0)
    args = [np.asarray(a, dtype=np.float32) for a in
            (g_pre, w_in, w_pool, pool_scale, conv_w, w_out, g_post, w_ple, w_ple_gate, g_ple)]
    y_all = _run(x_all, p_all, *args)
    return (np.ascontiguousarray(y_all[:nb]), np.ascontiguousarray(y_all[nb:]))
```

```python
import contextlib
import numpy as np
import concourse.bass as bass
import concourse.mybir as mybir
from concourse.bass_utils import run_bass_kernel_spmd

F32 = mybir.dt.float32
BF16 = mybir.dt.bfloat16
ALU = mybir.AluOpType
AF = mybir.ActivationFunctionType

N_CORES = 8
D = 1024
KD = 8
DIN = 3072
DPLE = 256
SEQ = 8192
CHUNK = SEQ // N_CORES
TB = 512
HAL = 8
CB = TB + 2 * HAL
EPS = 1e-6
WINDOWS = (2, 4, 8, 16)


class Sched:
    COMPUTE = ("pe", "act", "dve", "pool")

    def __init__(self):
        self.streams = {e: [] for e in self.COMPUTE + ("sp",)}
        self.cnt = {e: 0 for e in self.COMPUTE}
        self.waited = {e: {} for e in self.COMPUTE + ("sp",)}
        self.lastw = {}
        self.readers = {}
        self.dmacnt = {}
        self.const_keys = set()

    def _need(self, eng, tok, raw):
        sem, val, teng = tok
        if self.waited[eng].get(sem, 0) < val:
            self.streams[eng].append(("wait", sem, val))
            self.waited[eng][sem] = val

    @staticmethod
    def _is_psum(k):
        return k in ("trp", "misc") or (isinstance(k, tuple) and k[0] in ("zb", "cps"))

    def op(self, eng, fn, reads=(), writes=(), dma=None, dma_val=None):
        writes = list(writes) + [k for k in reads if self._is_psum(k) and k not in writes]
        reads = [k for k in reads if not self._is_psum(k)]
        for k in reads:
            if k in self.lastw:
                self._need(eng, self.lastw[k], True)
        for k in writes:
            if k in self.lastw:
                self._need(eng, self.lastw[k], False)
            for sem, (val, teng) in self.readers.get(k, {}).items():
                self._need(eng, (sem, val, teng), False)
        if dma is None:
            self.cnt[eng] += 1
            tok = ("E_" + eng, self.cnt[eng], eng)
            inc = 1
        else:
            self.dmacnt[dma] = self.dmacnt.get(dma, 0) + 16
            tok = (dma, dma_val if dma_val is not None else self.dmacnt[dma], "dma")
            inc = 16
        self.streams[eng].append(("op", fn, tok[0], inc))
        for k in reads:
            if k in self.const_keys:
                continue
            r = self.readers.setdefault(k, {})
            if r.get(tok[0], (0, None))[0] < tok[1]:
                r[tok[0]] = (tok[1], tok[2])
        for k in writes:
            self.lastw[k] = tok
            self.readers[k] = {}
        return tok

    def final_wait(self, eng, sem, val):
        if self.waited[eng].get(sem, 0) < val:
            self.streams[eng].append(("wait", sem, val))
            self.waited[eng][sem] = val

    def sem_names(self):
        names = set()
        for st in self.streams.values():
            for it in st:
                if it[0] == "wait":
                    names.add(it[1])
                else:
                    names.add(it[2])
        return sorted(names)


def build_program(NS):
    NB = 2 * NS
    nc = bass.Bass("TRN2", target_bir_lowering=False)
    dram = {}

    def din(name, shape):
        dram[name] = nc.dram_tensor(name, list(shape), F32, kind="ExternalInput").ap()
        return dram[name]

    xm = din("xm", [NB * TB, D])
    xh = din("xh", [NB * 16, D])
    pm = din("pm", [NB * TB, DPLE])
    w_in = din("w_in", [D, DIN])
    w_pool = din("w_pool", [4 * 128, 128])
    w_out = din("w_out", [D, D])
    w_gate = din("w_gate", [D, D])
    w_ple = din("w_ple", [DPLE, D])
    gpre_d = din("gpre", [128, KD])
    pscale_d = din("pscale", [128, 4])
    cw_d = din("cw", [128, 12])
    gpost_d = din("gpost", [128, D])
    gple_d = din("gple", [128, D])
    valid_d = din("valid", [128, 96])
    ident_d = din("ident", [128, 128])
    y = nc.dram_tensor("y", [NB * TB, D], F32, kind="ExternalOutput").ap()

    S = Sched()
    es = contextlib.ExitStack()

    def sb(name, shape, dt):
        return es.enter_context(nc.sbuf_tensor("sb_" + name, list(shape), dt))

    def ps(name, shape, dt):
        return es.enter_context(nc.psum_tensor("ps_" + name, list(shape), dt))

    Win = sb("Win", [128, KD, DIN], BF16)
    Wout = sb("Wout", [128, KD, D], BF16)
    Wg = sb("Wg", [128, KD, D], BF16)
    Wple = sb("Wple", [128, 2, D], BF16)
    Wpool = sb("Wpool", [128, 4, 128], BF16)
    gpost = sb("gpost_bc", [128, D], F32)
    gple = sb("gple_bc", [128, D], F32)
    ident_f = sb("ident_f", [128, 128], F32)
    ident = sb("ident", [128, 128], BF16)
    gpre = sb("gpre_t", [128, KD], F32)
    pscale = sb("pscale_t", [128, 4], F32)
    cw = sb("cw_t", [128, 12], F32)
    valid = sb("valid_t", [128, 4, 24], F32)
    vs0 = sb("vs0", [128, 4, 24], F32)
    vs1 = sb("vs1", [128, 4, 24], F32)
    invc = sb("invc", [128, 4, 4, 8], F32)
    neghalf = sb("neghalf", [128, 1], F32)
    stat = sb("stat", [128, 64], F32)
    junk = sb("junk", [128, D], BF16)

    NXA = 3
    NXC = 3
    xa = [sb(f"xa{i}", [128, D], F32) for i in range(NXA)]
    xc = [sb(f"xc{i}", [128, D], F32) for i in range(NXC)]
    hb = [sb(f"hb{i}", [128, D], BF16) for i in range(2)]
    hT = [sb(f"hT{i}", [128, KD, CB], BF16) for i in range(2)]
    abuf = [sb(f"a{i}", [128, CB], F32) for i in range(2)]
    sbuf_ = [sb(f"s{i}", [128, CB], F32) for i in range(2)]
    ccbuf = [sb(f"cc{i}", [128, CB], F32) for i in range(1)]
    vbuf = [sb(f"v{i}", [128, CB], F32) for i in range(2)]
    tbuf = [sb(f"t{i}", [128, TB], F32) for i in range(2)]
    etmp = sb("etmp", [128, 2, 8], F32)
    zhsb = [sb(f"zhsb{i}", [128, 12, 16], F32) for i in range(2)]
    dT = [sb(f"dT{i}", [128, TB], BF16) for i in range(2)]
    NSG = 3
    sg = [sb(f"sg{i}", [128, TB], F32) for i in range(NSG)]
    yT = [sb(f"yT{i}", [128, KD, TB], BF16) for i in range(2)]
    tmpc = sb("tmpc", [128, D], F32)
    x1b = [sb(f"x1b{i}", [128, D], BF16) for i in range(2)]
    x1T = [sb(f"x1T{i}", [128, KD, 128], BF16) for i in range(2)]
    th = tmpc
    NE = 1
    ebuf = [sb(f"e{i}", [128, D], F32) for i in range(NE)]
    pt = [sb(f"pt{i}", [128, DPLE], F32) for i in range(2)]
    pb = [sb(f"pb{i}", [128, DPLE], BF16) for i in range(2)]
    pT = [sb(f"pT{i}", [128, 2, 128], BF16) for i in range(2)]

    trp = ps("trp", [128, D], BF16)
    misc = ps("misc", [128, 512], F32)
    NZ = 4
    zb = [ps(f"zb{i}", [128, TB], F32) for i in range(NZ)]
    cps = ps("cps", [128, D], F32)
    zh = misc[:, 0:192].rearrange("p (i c) -> p i c", c=16)
    pTps = misc[:, 256:384].bitcast(BF16)

    state = {"z": 0, "stat": 0, "stg": 0}

    def next_z():
        i = state["z"] % NZ
        state["z"] += 1
        return i

    def stat_col():
        i = state["stat"] % 64
        state["stat"] += 1
        return i

    N_CONST = 7
    cst_total = 16 * N_CONST

    def cdma(dst, src, key):
        S.op("sp", lambda e, dst=dst, src=src: e.dma_start(out=dst, in_=src),
             writes=[key], dma="D_cst", dma_val=cst_total)

    cdma(gpre[:], gpre_d[:, :], "gpre")
    cdma(ident_f[:], ident_d[:, :], "ident_f")
    cdma(valid[:].rearrange("p a b -> p (a b)"), valid_d[:, :], "valid")
    cdma(pscale[:], pscale_d[:, :], "pscale")
    cdma(cw[:], cw_d[:, :], "cw")
    cdma(gpost[:], gpost_d[:, :], "gpost")
    cdma(gple[:], gple_d[:, :], "gple")
    S.op("pool", lambda e: e.memset(neghalf[:], -0.5), writes=["neghalf"])
    S.op("dve", lambda e: e.tensor_copy(out=ident[:], in_=ident_f[:]), reads=["ident_f"], writes=["ident"])

    def doubling(eng_name, src3, bufs3, n, g, key_src, key_bufs):
        cur, cur_key, length = src3, key_src, n
        for lvl in range(g + 1):
            sh = 1 << lvl
            dst = bufs3[lvl % 2]
            dk = key_bufs[lvl % 2]
            nl = length - sh
            S.op(eng_name,
                 lambda e, dst=dst, cur=cur, nl=nl, sh=sh: e.tensor_tensor(
                     out=dst(0, nl), in0=cur(0, nl), in1=cur(sh, sh + nl), op=ALU.add),
                 reads=[cur_key], writes=[dk])
            cur, cur_key, length = dst, dk, nl
        return cur, cur_key

    for g, w in enumerate(WINDOWS):
        left = (w - 1) // 2
        src = lambda a, b: valid[:, :, a:b]
        b0 = lambda a, b: vs0[:, :, a:b]
        b1 = lambda a, b: vs1[:, :, a:b]
        cur, ck = doubling("dve", src, [b0, b1], 24, g, "valid", ["vs0", "vs1"])
        S.op("dve", lambda e, cur=cur, g=g, left=left: e.reciprocal(
            out=invc[:, g, :, :], in_=cur(8 - left, 16 - left)), reads=[ck], writes=["invc"])

    cast_rr = {"i": 0}

    def WK(name):
        return [(name, en) for en in ("dve", "pool", "act")]

    def stage_weight(wname, src_ap, dst_ap, rows, cols, scale_ap=None, wide=False):
        pool_ = [("xc", i) for i in range(NXC)] + ([("xa", i) for i in range(NXA)] if wide else [])
        kind, slot = pool_[state["stg"] % len(pool_)]
        state["stg"] += 1
        stg = (xc if kind == "xc" else xa)[slot]
        S.op("sp", lambda e, stg=stg, src_ap=src_ap, rows=rows, cols=cols: e.dma_start(
            out=stg[0:rows, 0:cols], in_=src_ap), writes=[(kind, slot)], dma=f"D_{kind}{slot}")
        eng = ("dve", "pool", "act")[cast_rr["i"] % 3]
        cast_rr["i"] += 1
        rd = [(kind, slot)] + (["gpre"] if scale_ap is not None else [])
        if scale_ap is not None:
            if eng == "act":
                fn = lambda e, stg=stg, dst_ap=dst_ap, rows=rows, cols=cols, scale_ap=scale_ap: e.activation(
                    out=dst_ap, in_=stg[0:rows, 0:cols], func=AF.Copy, scale=scale_ap)
            else:
                fn = lambda e, stg=stg, dst_ap=dst_ap, rows=rows, cols=cols, scale_ap=scale_ap: e.tensor_scalar(
                    out=dst_ap, in0=stg[0:rows, 0:cols], scalar1=scale_ap, scalar2=1.0, op0=ALU.mult, op1=ALU.mult)
        else:
            if eng == "act":
                fn = lambda e, stg=stg, dst_ap=dst_ap, rows=rows, cols=cols: e.activation(
                    out=dst_ap, in_=stg[0:rows, 0:cols], func=AF.Copy)
            else:
                fn = lambda e, stg=stg, dst_ap=dst_ap, rows=rows, cols=cols: e.tensor_copy(
                    out=dst_ap, in_=stg[0:rows, 0:cols])
        S.op(eng, fn, reads=rd, writes=[(wname, eng)])

    def stage_early_weights():
        for n in (2, 1, 0):
            for k in range(KD):
                stage_weight(f"Win{n}", w_in[k * 128:(k + 1) * 128, n * 1024:(n + 1) * 1024],
                             Win[:, k, n * 1024:(n + 1) * 1024], 128, 1024, gpre[:, k:k + 1], wide=True)
        for g in range(4):
            stage_weight("Wpool", w_pool[g * 128:(g + 1) * 128, :], Wpool[:, g, :], 128, 128, wide=True)
        state["stg"] = 0

    late_weights = []
    for k in range(KD):
        late_weights.append(("Wout", w_out[k * 128:(k + 1) * 128, :], Wout[:, k, :], 128, 1024))
    for k in range(KD):
        late_weights.append(("Wg", w_gate[k * 128:(k + 1) * 128, :], Wg[:, k, :], 128, 1024))
    for k in range(2):
        late_weights.append(("Wple", w_ple[k * 128:(k + 1) * 128, :], Wple[:, k, :], 128, 1024))
    for _wn in ("Win0", "Win1", "Win2", "Wpool", "Wout", "Wg", "Wple"):
        S.const_keys.update(WK(_wn))
    S.const_keys.update(["gpre", "pscale", "cw", "gpost", "gple", "ident", "invc", "neghalf"])

    cnt = {"xa": 0, "hb": 0, "xc": 0, "pt": 0, "pb": 0, "x1b": 0, "e": 0, "ab": 0, "vb": 0,
           "dT": 0, "sg": 0}

    def rot(name, n):
        i = cnt[name] % n
        cnt[name] += 1
        return i

    def rstd_from(ss_ap, ss_key, eps_val, extra=None):
        c1 = stat_col()
        c2 = stat_col()
        m_ap = stat[:, c1:c1 + 1]
        r_ap = stat[:, c2:c2 + 1]
        mk, rk = ("stat", c1), ("stat", c2)
        if extra is None:
            S.op("pool", lambda e: e.tensor_scalar(out=m_ap, in0=ss_ap, scalar1=float(eps_val), scalar2=None,
                                                   op0=ALU.add), reads=[ss_key], writes=[mk])
        else:
            ex_ap, ex_key = extra
            S.op("pool", lambda e: e.tensor_scalar(out=m_ap, in0=ss_ap, scalar1=ex_ap, scalar2=float(eps_val),
                                                   op0=ALU.add, op1=ALU.add), reads=[ss_key, ex_key], writes=[mk])
        S.op("pool", lambda e: e.tensor_tensor(out=r_ap, in0=m_ap, in1=neghalf[:], op=ALU.pow),
             reads=[mk, "neghalf"], writes=[rk])
        return r_ap, rk

    actx = {}

    aload = {}
    cload = {}

    def A_load(b, j):
        rows = 128 if j < 4 else 16
        xs = rot("xa", NXA)
        xt = xa[xs]
        src = xm[b * TB + j * 128: b * TB + (j + 1) * 128, :] if j < 4 else xh[b * 16:(b + 1) * 16, :]
        S.op("sp", lambda e: e.dma_start(out=xt[0:rows, :], in_=src), writes=[("xa", xs)], dma=f"D_xa{xs}")
        aload[(b, j)] = xs

    def C_load(b, j):
        r0 = b * TB + j * 128
        xs = rot("xc", NXC)
        pi = rot("pt", 2)
        S.op("sp", lambda e: e.dma_start(out=xc[xs][:, :], in_=xm[r0:r0 + 128, :]), writes=[("xc", xs)],
             dma=f"D_xc{xs}")
        S.op("sp", lambda e: e.dma_start(out=pt[pi][:, :], in_=pm[r0:r0 + 128, :]), writes=[("pt", pi)],
             dma=f"D_pt{pi}")
        cload[(b, j)] = (xs, pi)

    def stage_A1(b, j):
        rows = 128 if j < 4 else 16
        xs = aload.pop((b, j))
        xt = xa[xs]
        c0 = stat_col()
        ss_ap, ssk = stat[0:rows, c0:c0 + 1], ("stat", c0)
        S.op("act", lambda e: e.activation(out=junk[0:rows, :], in_=xt[0:rows, :], func=AF.Square,
                                           scale=1.0 / 32.0, accum_out=ss_ap),
             reads=[("xa", xs)], writes=[ssk, ("junk", 0), ("junk", 1)])
        c1, c2 = stat_col(), stat_col()
        m_ap, r_ap = stat[0:rows, c1:c1 + 1], stat[0:rows, c2:c2 + 1]
        mk, rk = ("stat", c1), ("stat", c2)
        S.op("pool", lambda e: e.tensor_scalar(out=m_ap, in0=ss_ap, scalar1=float(EPS), scalar2=None,
                                               op0=ALU.add), reads=[ssk], writes=[mk])
        S.op("pool", lambda e: e.tensor_tensor(out=r_ap, in0=m_ap, in1=neghalf[0:rows, :], op=ALU.pow),
             reads=[mk, "neghalf"], writes=[rk])
        actx[(b, j)] = dict(xs=xs, r_ap=r_ap, rk=rk)

    a2ctx = {}

    def stage_A2a(b, j):
        hs = b % 2
        rows = 128 if j < 4 else 16
        c = actx.pop((b, j))
        xs, r_ap, rk = c["xs"], c["r_ap"], c["rk"]
        xt = xa[xs]
        hi = rot("hb", 2)
        hbt = hb[hi]
        S.op("act", lambda e: e.activation(out=hbt[0:rows, :], in_=xt[0:rows, :], func=AF.Copy, scale=r_ap),
             reads=[("xa", xs), rk], writes=[("hb", hi)])
        a2ctx[(b, j)] = hi

    def stage_A2b(b, j):
        hs = b % 2
        rows = 128 if j < 4 else 16
        hi = a2ctx.pop((b, j))
        hbt = hb[hi]

        def tr(e):
            ins = None
            for k in range(KD):
                ins = e.transpose(out=trp[:, k * 128:k * 128 + rows],
                                  in_=hbt[0:rows, k * 128:(k + 1) * 128], identity=ident[0:rows, 0:rows])
            return ins
        S.op("pe", tr, reads=[("hb", hi), "ident"], writes=["trp"])
        src_v = trp[:, :].rearrange("p (k t) -> p k t", t=128)[:, :, 0:rows]
        if j < 4:
            dst_v = hT[hs][:, :, j * 128:(j + 1) * 128]
        else:
            dst_v = hT[hs][:, :, TB:TB + 16]
        S.op("dve", lambda e: e.tensor_copy(out=dst_v, in_=src_v), reads=["trp"], writes=[("hT", hs, j)])

    HT_KEYS = lambda hs: [("hT", hs, j) for j in range(5)]

    def zgroup(b, ec):
        hs = b % 2
        zi = next_z()

        def mm(e):
            ins = None
            for k in range(KD):
                ins = e.matmul(zb[zi][:, :], lhsT=Win[:, k, ec * 128:(ec + 1) * 128], rhs=hT[hs][:, k, 0:TB],
                               start=(k == 0), stop=(k == KD - 1))
            return ins
        S.op("pe", mm, reads=WK(f"Win{ec // 8}") + HT_KEYS(hs)[:4], writes=[("zb", zi)])
        return zi

    HALO_EC = [0, 1, 2, 3, 8, 9, 10, 11, 16, 17, 18, 19]

    def stage_B_halo(b):
        hs = b % 2

        def mmh(e):
            ins = None
            for idx, ec in enumerate(HALO_EC):
                for k in range(KD):
                    ins = e.matmul(zh[:, idx, :], lhsT=Win[:, k, ec * 128:(ec + 1) * 128],
                                   rhs=hT[hs][:, k, TB:TB + 16], start=(k == 0), stop=(k == KD - 1))
            return ins
        S.op("pe", mmh, reads=WK("Win0") + WK("Win1") + WK("Win2") + [("hT", hs, 4)], writes=["misc"])
        S.op("act", lambda e: e.activation(out=zhsb[hs][:, :, :], in_=zh[:, :, :], func=AF.Copy),
             reads=["misc"], writes=[("zhsb", hs)])

    def halo_view(hs, idx):
        return zhsb[hs][:, idx, :].rearrange("p (a c) -> p a c", c=8)

    def edges_nat(buf):
        return buf[:, :].rearrange("p (a c) -> p a c", c=8)[:, 0:66:65, :]

    def edges_main(ap512):
        return ap512.rearrange("p (a c) -> p a c", c=8)[:, 0:64:63, :]

    pctx = {}

    def stage_B_pool1a(b, g):
        hs = b % 2
        h = b % 2
        w = WINDOWS[g]
        left = (w - 1) // 2
        za = zgroup(b, g)
        ai = rot("ab", 2)
        a = abuf[ai]
        ak = ("a", ai)
        S.op("act", lambda e: e.activation(out=a[:, HAL:HAL + TB], in_=zb[za][:, :], func=AF.Copy),
             reads=[("zb", za)], writes=[ak])
        S.op("pool", lambda e: e.tensor_copy(out=edges_nat(a), in_=halo_view(hs, g)),
             reads=[("zhsb", hs)], writes=[(ak, "e")])
        src = lambda lo, hi: a[:, lo:hi]
        b0 = lambda lo, hi: sbuf_[0][:, lo:hi]
        b1 = lambda lo, hi: sbuf_[1][:, lo:hi]
        cur, cur_key, length = src, None, CB
        first = True
        for lvl in range(g + 1):
            sh = 1 << lvl
            dst = (b0, b1)[lvl % 2]
            dk = ("s", lvl % 2)
            nl = length - sh
            rd = [ak, (ak, "e")] if first else [cur_key]
            S.op("dve", lambda e, dst=dst, cur=cur, nl=nl, sh=sh: e.tensor_tensor(
                out=dst(0, nl), in0=cur(0, nl), in1=cur(sh, sh + nl), op=ALU.add), reads=rd, writes=[dk])
            cur, cur_key, length, first = dst, dk, nl, False
        o = HAL - left
        di = rot("dT", 2)
        S.op("dve", lambda e: e.scalar_tensor_tensor(out=dT[di][:, :], in0=cur(o, o + TB), scalar=1.0 / w,
                                                     in1=a[:, HAL:HAL + TB], op0=ALU.mult, op1=ALU.subtract),
             reads=[cur_key, ak], writes=[("dT", di)])
        S.op("pool", lambda e: e.tensor_tensor(out=etmp[:, :, :], in0=edges_main(cur(o, o + TB)),
                                               in1=invc[:, g, 2 * h:2 * h + 2, :], op=ALU.mult),
             reads=[cur_key, "invc"], writes=["etmp"])
        S.op("pool", lambda e: e.tensor_tensor(out=edges_main(dT[di][:, :]), in0=etmp[:, :, :],
                                               in1=edges_main(a[:, HAL:HAL + TB]), op=ALU.subtract),
             reads=["etmp", ak, ("dT", di)], writes=[("dT", di)])
        pctx[(b, g)] = di

    def stage_B_pool1b(b, g):
        di = pctx[(b, g)]
        zg = zgroup(b, 4 + g)
        si = rot("sg", NSG)
        S.op("act", lambda e: e.activation(out=sg[si][:, :], in_=zb[zg][:, :], func=AF.Silu),
             reads=[("zb", zg)], writes=[("sg", si)])
        pctx[(b, g)] = (di, si)

    def stage_B_pool2(b, g):
        hs = b % 2
        di, si = pctx.pop((b, g))
        zp = next_z()
        S.op("pe", lambda e: e.matmul(zb[zp][:, :], lhsT=Wpool[:, g, :], rhs=dT[di][:, :], start=True, stop=True),
             reads=WK("Wpool") + [("dT", di)], writes=[("zb", zp)])
        S.op("dve", lambda e: e.scalar_tensor_tensor(out=yT[hs][:, g, :], in0=zb[zp][:, :],
                                                     scalar=pscale[:, g:g + 1], in1=sg[si][:, :],
                                                     op0=ALU.mult, op1=ALU.mult),
             reads=[("zb", zp), ("sg", si), "pscale"], writes=[("yT", hs, g)])

    vctx = {}

    def stage_B_conv_a(b, g):
        hs = b % 2
        zc = zgroup(b, 16 + g)
        cc = ccbuf[0]
        S.op("act", lambda e: e.activation(out=cc[:, HAL:HAL + TB], in_=zb[zc][:, :], func=AF.Copy),
             reads=[("zb", zc)], writes=["cc"])
        zu = zgroup(b, 8 + g)
        vi = rot("vb", 2)
        v = vbuf[vi]
        vk = ("v", vi)
        S.op("pool", lambda e: e.tensor_tensor(out=edges_nat(v), in0=halo_view(hs, 4 + g), in1=halo_view(hs, 8 + g),
                                               op=ALU.mult), reads=[("zhsb", hs)], writes=[(vk, "e")])
        vctx[(b, g)] = dict(zu=zu, vi=vi)

    def stage_B_conv_b1(b, g):
        c = vctx[(b, g)]
        zu, vi = c["zu"], c["vi"]
        cc = ccbuf[0]
        v = vbuf[vi]
        vk = ("v", vi)
        S.op("dve", lambda e: e.tensor_tensor(out=v[:, HAL:HAL + TB], in0=zb[zu][:, :], in1=cc[:, HAL:HAL + TB],
                                              op=ALU.mult), reads=[("zb", zu), "cc"], writes=[vk])
        w0, w1, w2 = (cw[:, 3 * g + i:3 * g + i + 1] for i in range(3))
        S.op("dve", lambda e: e.tensor_scalar(out=tbuf[0][:, :], in0=v[:, HAL:HAL + TB], scalar1=w1, scalar2=None,
                                              op0=ALU.mult), reads=[vk, "cw"], writes=[("t", 0)])
        S.op("dve", lambda e: e.scalar_tensor_tensor(out=tbuf[1][:, :], in0=v[:, HAL - 1:HAL - 1 + TB], scalar=w0,
                                                     in1=tbuf[0][:, :], op0=ALU.mult, op1=ALU.add),
             reads=[vk, (vk, "e"), ("t", 0), "cw"], writes=[("t", 1)])
        S.op("dve", lambda e: e.scalar_tensor_tensor(out=tbuf[0][:, :], in0=v[:, HAL + 1:HAL + 1 + TB], scalar=w2,
                                                     in1=tbuf[1][:, :], op0=ALU.mult, op1=ALU.add),
             reads=[vk, (vk, "e"), ("t", 1), "cw"], writes=[("t", 0)])

    def stage_B_conv_b2(b, g):
        hs = b % 2
        vctx.pop((b, g))
        zg = zgroup(b, 20 + g)
        si = rot("sg", NSG)
        S.op("act", lambda e: e.activation(out=sg[si][:, :], in_=zb[zg][:, :], func=AF.Silu),
             reads=[("zb", zg)], writes=[("sg", si)])
        zB = zgroup(b, 12 + g)
        S.op("dve", lambda e: e.tensor_tensor(out=tbuf[1][:, :], in0=zb[zB][:, :], in1=tbuf[0][:, :], op=ALU.mult),
             reads=[("zb", zB), ("t", 0)], writes=[("t", 1)])
        S.op("pool", lambda e: e.tensor_tensor(out=yT[hs][:, 4 + g, :], in0=tbuf[1][:, :], in1=sg[si][:, :],
                                               op=ALU.mult),
             reads=[("t", 1), ("sg", si)], writes=[("yT", hs, 4 + g)])

    YT_KEYS = lambda hs: [("yT", hs, g) for g in range(8)]
    cctx = {}

    def stage_C1a(b, j):
        hs = b % 2
        r0 = b * TB + j * 128
        xs, pi = cload.pop((b, j))
        xt = xc[xs]
        sscols = []
        for hf in range(2):
            def mm(e, hf=hf):
                ins = None
                for k in range(KD):
                    ins = e.matmul(cps[:, hf * 512:(hf + 1) * 512], lhsT=yT[hs][:, k, j * 128:(j + 1) * 128],
                                   rhs=Wout[:, k, hf * 512:(hf + 1) * 512], start=(k == 0), stop=(k == KD - 1))
                return ins
            S.op("pe", mm, reads=WK("Wout") + YT_KEYS(hs), writes=[("cps", hf)])
        for hf in range(2):
            c0 = stat_col()
            ss_ap, ssk = stat[:, c0:c0 + 1], ("stat", c0)
            S.op("act", lambda e, hf=hf, ss_ap=ss_ap: e.activation(
                out=junk[:, hf * 512:(hf + 1) * 512], in_=cps[:, hf * 512:(hf + 1) * 512], func=AF.Square,
                scale=1.0 / 32.0, accum_out=ss_ap), reads=[("cps", hf)], writes=[ssk, ("junk", hf)])
            sscols.append((ss_ap, ssk))
            sl = slice(hf * 512, (hf + 1) * 512)
            S.op("dve", lambda e, sl=sl: e.tensor_tensor(out=tmpc[:, sl], in0=cps[:, sl], in1=gpost[:, sl],
                                                        op=ALU.mult),
                 reads=[("cps", hf), "gpost"], writes=[("tmpc", hf)])
        r_ap, rk = rstd_from(sscols[0][0], sscols[0][1], EPS, extra=sscols[1])
        cctx[(b, j)] = dict(xs=xs, pi=pi, r0=r0, r_ap=r_ap, rk=rk)

    def stage_C1b(b, j):
        c = cctx[(b, j)]
        xs, r_ap, rk = c["xs"], c["r_ap"], c["rk"]
        xt = xc[xs]
        S.op("dve", lambda e: e.scalar_tensor_tensor(out=xt[:, :], in0=tmpc[:, :], scalar=r_ap, in1=xt[:, :],
                                                     op0=ALU.mult, op1=ALU.add),
             reads=[("tmpc", 0), ("tmpc", 1), rk, ("xc", xs)], writes=[("xc", xs)])
        xi = rot("x1b", 2)
        S.op("act", lambda e: e.activation(out=x1b[xi][:, :], in_=xt[:, :], func=AF.Copy), reads=[("xc", xs)],
             writes=[("x1b", xi)])
        c["xi"] = xi
        pi = c["pi"]
        bi = rot("pb", 2)
        S.op("act", lambda e: e.activation(out=pb[bi][:, :], in_=pt[pi][:, :], func=AF.Copy), reads=[("pt", pi)],
             writes=[("pb", bi)])
        c["bi"] = bi

    def stage_C2(b, j):
        c = cctx[(b, j)]
        xi, bi = c["xi"], c["bi"]

        def tr(e):
            ins = None
            for k in range(KD):
                ins = e.transpose(out=trp[:, k * 128:(k + 1) * 128], in_=x1b[xi][:, k * 128:(k + 1) * 128],
                                  identity=ident[:, :])
            return ins
        S.op("pe", tr, reads=[("x1b", xi), "ident"], writes=["trp"])
        S.op("act", lambda e: e.activation(out=x1T[xi][:, :, :].rearrange("p k t -> p (k t)"), in_=trp[:, :],
                                           func=AF.Copy), reads=["trp"], writes=[("x1T", xi)])

        def trp2(e):
            ins = None
            for k in range(2):
                ins = e.transpose(out=pTps[:, k * 128:(k + 1) * 128], in_=pb[bi][:, k * 128:(k + 1) * 128],
                                  identity=ident[:, :])
            return ins
        S.op("pe", trp2, reads=[("pb", bi), "ident"], writes=["misc"])
        S.op("dve", lambda e: e.tensor_copy(out=pT[bi][:, :, :].rearrange("p k t -> p (k t)"), in_=pTps[:, :]),
             reads=["misc"], writes=[("pT", bi)])

    def stage_C3a(b, j):
        c = cctx.pop((b, j))
        xs, xi, bi, r0 = c["xs"], c["xi"], c["bi"], c["r0"]
        xt = xc[xs]
        for hf in range(2):
            def mmg(e, hf=hf):
                ins = None
                for k in range(KD):
                    ins = e.matmul(cps[:, hf * 512:(hf + 1) * 512], lhsT=x1T[xi][:, k, :],
                                   rhs=Wg[:, k, hf * 512:(hf + 1) * 512], start=(k == 0), stop=(k == KD - 1))
                return ins
            S.op("pe", mmg, reads=WK("Wg") + [("x1T", xi)], writes=[("cps", hf)])
        for hf in range(2):
            sl = slice(hf * 512, (hf + 1) * 512)
            S.op("act", lambda e, sl=sl: e.activation(out=th[:, sl], in_=cps[:, sl], func=AF.Tanh, scale=0.5),
                 reads=[("cps", hf)], writes=[("tmpc", hf)])
        cctx[(b, j)] = c

    def stage_C3b(b, j):
        c = cctx.pop((b, j))
        xs, xi, bi, r0 = c["xs"], c["xi"], c["bi"], c["r0"]
        xt = xc[xs]
        for hf in range(2):
            def mmp(e, hf=hf):
                ins = None
                for k in range(2):
                    ins = e.matmul(cps[:, hf * 512:(hf + 1) * 512], lhsT=pT[bi][:, k, :],
                                   rhs=Wple[:, k, hf * 512:(hf + 1) * 512], start=(k == 0), stop=(k == 1))
                return ins
            S.op("pe", mmp, reads=WK("Wple") + [("pT", bi)], writes=[("cps", hf)])
        ei = rot("e", NE)
        eb = ebuf[ei]
        sscols = []
        for hf in range(2):
            sl = slice(hf * 512, (hf + 1) * 512)
            S.op("dve", lambda e, sl=sl: e.scalar_tensor_tensor(out=eb[:, sl], in0=th[:, sl], scalar=1.0,
                                                               in1=cps[:, sl], op0=ALU.add, op1=ALU.mult),
                 reads=[("tmpc", hf), ("cps", hf)], writes=[("e", ei, hf)])
            c0 = stat_col()
            ss_ap, ssk = stat[:, c0:c0 + 1], ("stat", c0)
            S.op("act", lambda e, sl=sl, ss_ap=ss_ap, hf=hf: e.activation(
                out=junk[:, sl], in_=eb[:, sl], func=AF.Square, scale=1.0 / 32.0, accum_out=ss_ap),
                reads=[("e", ei, hf)], writes=[ssk, ("junk", hf)])
            sscols.append((ss_ap, ssk))
        r_ap, rk = rstd_from(sscols[0][0], sscols[0][1], 4.0 * EPS, extra=sscols[1])
        for hf in range(2):
            sl = slice(hf * 512, (hf + 1) * 512)
            S.op("dve", lambda e, sl=sl: e.tensor_tensor(out=eb[:, sl], in0=eb[:, sl], in1=gple[:, sl], op=ALU.mult),
                 reads=[("e", ei, hf), "gple"], writes=[("e", ei, hf)])
        S.op("dve", lambda e: e.scalar_tensor_tensor(out=xt[:, :], in0=eb[:, :], scalar=r_ap, in1=xt[:, :],
                                                     op0=ALU.mult, op1=ALU.add),
             reads=[("e", ei, 0), ("e", ei, 1), rk, ("xc", xs)], writes=[("xc", xs)])
        S.op("sp", lambda e: e.dma_start(out=y[r0:r0 + 128, :], in_=xt[:, :]), reads=[("xc", xs)],
             dma=f"D_st{xs}", writes=[("ystore", xs)])

    def a_tiles(b, j):
        return [(b, 4), (b, 0)] if j == 0 else [(b, j)]

    for t in [(0, 4), (0, 0), (0, 1), (0, 2), (0, 3)]:
        A_load(*t)
        stage_A1(*t)
        stage_A2a(*t)
        stage_A2b(*t)
    stage_early_weights()
    if NB > 1:
        for t in a_tiles(1, 0):
            A_load(*t)
    for s in range(NB + 1):
        hasA, hasB, hasC = (s + 1 < NB), (s < NB), (s >= 1)
        if s == NB:
            b = NB - 1
            stage_C1a(b, 0)
            stage_C1b(b, 0)
            for j in range(4):
                if j < 3:
                    C_load(b, j + 1)
                    stage_C1a(b, j + 1)
                    stage_C1b(b, j + 1)
                stage_C2(b, j)
                stage_C3a(b, j)
                stage_C3b(b, j)
            break
        for j in range(4):
            now = a_tiles(s + 1, j) if hasA else []
            if j < 3:
                nxt = a_tiles(s + 1, j + 1) if hasA else []
            else:
                nxt = a_tiles(s + 2, 0) if s + 2 < NB else []
            for t in nxt:
                A_load(*t)
            if j < 3:
                if hasC:
                    C_load(s - 1, j + 1)
            elif 0 < s < NB:
                C_load(s, 0)
            for t in now:
                stage_A1(*t)
            if hasB and j == 0:
                stage_B_halo(s)
            if hasB:
                stage_B_conv_a(s, j)
                stage_B_conv_b1(s, j)
            if hasC:
                stage_C1a(s - 1, j)
            for t in now:
                stage_A2a(*t)
            if hasC:
                stage_C1b(s - 1, j)
            if hasB:
                stage_B_conv_b2(s, j)
            if s == 0:
                for wargs in late_weights[j * 5:(j + 1) * 5]:
                    stage_weight(*wargs)
            if not hasC:
                for t in now:
                    stage_A2b(*t)
            if hasB:
                stage_B_pool1a(s, j)
            if hasC:
                stage_C2(s - 1, j)
            if hasB:
                stage_B_pool1b(s, j)
            if hasC:
                stage_C3a(s - 1, j)
                for t in now:
                    stage_A2b(*t)
                if hasB:
                    stage_B_pool2(s, j)
                stage_C3b(s - 1, j)
            elif hasB:
                stage_B_pool2(s, j)
            if s == 0 and j == 3 and NB > 0:
                C_load(0, 0)

    for name, val in sorted(S.dmacnt.items()):
        if name.startswith("D_st"):
            S.final_wait("sp", name, val)

    sems = {}
    for name in S.sem_names():
        sems[name] = es.enter_context(nc.semaphore(name))

    def replay(eng_name, eng):
        for it in S.streams[eng_name]:
            if it[0] == "wait":
                eng.wait_ge(sems[it[1]], it[2])
            else:
                ins = it[1](eng)
                ins.then_inc(sems[it[2]], it[3])

    with nc.Block() as block:
        @block.sync
        def _(e):
            replay("sp", e)

        @block.tensor
        def _(e):
            replay("pe", e)

        @block.scalar
        def _(e):
            replay("act", e)

        @block.vector
        def _(e):
            replay("dve", e)

        @block.gpsimd
        def _(e):
            replay("pool", e)

    es.close()
    return nc


_PROG_CACHE = {}


def _get_program(NS):
    if NS not in _PROG_CACHE:
        _PROG_CACHE[NS] = build_program(NS)
    return _PROG_CACHE[NS]


def _run(x_all, p_all, g_pre, w_in, w_pool, pool_scale, conv_w, w_out, g_post, w_ple, w_ple_gate, g_ple):
    NS = x_all.shape[0]
    NB = 2 * NS
    f32 = np.float32
    nc = _get_program(NS)
    shared = {
        "w_in": np.ascontiguousarray(w_in[0], dtype=f32),
        "w_pool": np.ascontiguousarray(w_pool[0].reshape(4 * 128, 128), dtype=f32),
        "w_out": np.ascontiguousarray(w_out[0], dtype=f32),
        "w_gate": np.ascontiguousarray(w_ple_gate[0], dtype=f32),
        "w_ple": np.ascontiguousarray(w_ple[0], dtype=f32),
        "gpre": np.ascontiguousarray(g_pre[0].reshape(KD, 128).T, dtype=f32),
        "pscale": np.ascontiguousarray(pool_scale[0].reshape(4, 128).T, dtype=f32),
        "cw": np.ascontiguousarray(conv_w[0].reshape(3, 4, 128).transpose(2, 1, 0).reshape(128, 12), dtype=f32),
        "gpost": np.ascontiguousarray(np.broadcast_to(g_post[0][None, :], (128, D)), dtype=f32),
        "gple": np.ascontiguousarray(np.broadcast_to(g_ple[0][None, :], (128, D)), dtype=f32),
        "ident": np.eye(128, dtype=f32),
    }
    in_maps = []
    for c in range(N_CORES):
        xm = np.ascontiguousarray(x_all[:, c * CHUNK:(c + 1) * CHUNK, :]).reshape(NS * CHUNK, D)
        pm = np.ascontiguousarray(p_all[:, c * CHUNK:(c + 1) * CHUNK, :]).reshape(NS * CHUNK, DPLE)
        xh = np.zeros((NS, 2, 16, D), dtype=f32)
        valid = np.zeros((4, 24), dtype=f32)
        for h in range(2):
            start = c * CHUNK + h * TB
            pos = np.concatenate([np.arange(start - HAL, start), np.arange(start + TB, start + TB + HAL)])
            ok = (pos >= 0) & (pos < SEQ)
            if ok.any():
                xh[:, h, ok, :] = x_all[:, pos[ok], :]
            for side in range(2):
                nat = np.arange(0, 24) if side == 0 else np.arange(CB - 24, CB)
                pp = start + nat - HAL
                valid[2 * h + side] = ((pp >= 0) & (pp < SEQ)).astype(f32)
        m = dict(shared)
        m["xm"] = xm
        m["pm"] = pm
        m["xh"] = xh.reshape(NB * 16, D)
        m["valid"] = np.ascontiguousarray(np.broadcast_to(valid.reshape(1, 96), (128, 96)), dtype=f32)
        in_maps.append(m)
    res = run_bass_kernel_spmd(nc, in_maps, core_ids=list(range(N_CORES)))
    y_all = np.empty((NS, SEQ, D), dtype=f32)
    for c in range(N_CORES):
        y_all[:, c * CHUNK:(c + 1) * CHUNK, :] = np.asarray(res.results[c]["y"]).reshape(NS, CHUNK, D)
    return y_all


def kernel(x_prompt, x_sample, p_prompt, p_sample, g_pre, w_in, w_pool, pool_scale, conv_w, w_out, g_post,
           w_ple, w_ple_gate, g_ple):
    x_prompt = np.asarray(x_prompt, dtype=np.float32)
    x_sample = np.asarray(x_sample, dtype=np.float32)
    nb = x_prompt.shape[0]
    x_all = np.concatenate([x_prompt, x_sample], axis=0)
    p_all = np.concatenate([np.asarray(p_prompt, dtype=np.float32)[0], np.asarray(p_sample, dtype=np.float32)[0]],
                           axis=0)
    args = [np.asarray(a, dtype=np.float32) for a in
            (g_pre, w_in, w_pool, pool_scale, conv_w, w_out, g_post, w_ple, w_ple_gate, g_ple)]
    y_all = _run(x_all, p_all, *args)
    return (np.ascontiguousarray(y_all[:nb]), np.ascontiguousarray(y_all[nb:]))
```
